# Optimizing a Trainium2 kernel written in Bass

```python
import math
import jax
import jax.numpy as jnp
from jax import lax
import numpy as np

D_MODEL = 4096
BATCH = 2
SEQ = 4096
DEPTH = 4

HD_A = 64
H_A = D_MODEL // 128
H_KV = H_A // 8
GROUP = H_A // H_KV
WIDTH_A = H_A * HD_A
WINDOW = 128
BLK = WINDOW
H_B = D_MODEL // 256
DK = 128
DV = 128
WIDTH_B = H_B * DV
CONV_K = 4
CHUNK = 64
C_CONV = 2 * H_B * DK + H_B * DV
EPS = 1e-6

SPLIT_SIZES = (WIDTH_A, H_KV * HD_A, H_KV * HD_A, WIDTH_A, C_CONV, WIDTH_B, H_B, H_B, D_MODEL, D_MODEL)
N_IN = 2 * WIDTH_A + 2 * H_KV * HD_A + C_CONV + WIDTH_B + 2 * H_B + 2 * D_MODEL

kernel_name = 'hybrid_swa_sink_alibi_gated_deltanet_gated_merge'


def rms_norm(x, g):
    xf = x.astype(jnp.float32)
    y = xf * lax.rsqrt(jnp.mean(xf * xf, axis=-1, keepdims=True) + EPS)
    return (y * g.astype(jnp.float32)).astype(x.dtype)


def l2_normalize(x):
    xf = x.astype(jnp.float32)
    return xf * lax.rsqrt(jnp.sum(xf * xf, axis=-1, keepdims=True) + EPS)


def alibi_slopes():
    return jnp.exp2(-8.0 * jnp.arange(1, H_A + 1, dtype=jnp.float32) / H_A)


def sliding_window_attention(q, k, v, sinks):
    b, t = q.shape[0], q.shape[1]
    nb = t // BLK
    qb = q.reshape(b, nb, BLK, H_KV, GROUP, HD_A)
    kb = k.reshape(b, nb, BLK, H_KV, HD_A)
    vb = v.reshape(b, nb, BLK, H_KV, HD_A)
    pad = ((0, 0), (1, 0), (0, 0), (0, 0), (0, 0))
    k_band = jnp.concatenate([jnp.pad(kb, pad)[:, :-1], kb], axis=2)
    v_band = jnp.concatenate([jnp.pad(vb, pad)[:, :-1], vb], axis=2)
    s = jnp.einsum('bnqhgd,bnkhd->bnhgqk', qb, k_band).astype(jnp.float32) * (HD_A ** -0.5)
    qi = jnp.arange(BLK)[:, None]
    kj = jnp.arange(2 * BLK)[None, :]
    dist = qi + BLK - kj
    key_pos = jnp.arange(nb)[:, None, None] * BLK - BLK + kj[None]
    valid = (dist >= 0)[None] & (dist < WINDOW)[None] & (key_pos >= 0)
    slopes = alibi_slopes().reshape(H_KV, GROUP)[:, :, None, None]
    s = s - slopes * dist.astype(jnp.float32)
    s = jnp.where(valid[None, :, None, None], s, -jnp.inf)
    sink = jnp.broadcast_to(sinks.astype(jnp.float32).reshape(H_KV, GROUP, 1, 1), s.shape[:-1] + (1,))
    p = jax.nn.softmax(jnp.concatenate([s, sink], axis=-1), axis=-1)[..., :-1]
    o = jnp.einsum('bnhgqk,bnkhd->bnqhgd', p.astype(v.dtype), v_band)
    return o.reshape(b, t, WIDTH_A)


def causal_depthwise_conv(x, w):
    c = x.shape[-1]
    return lax.conv_general_dilated(
        x, w[:, None, :].astype(x.dtype), window_strides=(1,), padding=((CONV_K - 1, 0),),
        dimension_numbers=('NWC', 'WIO', 'NWC'), feature_group_count=c)


def gated_delta_rule(q, k, v, beta, g):
    b, t = q.shape[0], q.shape[1]
    n = t // CHUNK
    f32 = jnp.float32

    def chunks(a):
        a = a.astype(f32).reshape((b, n, CHUNK) + a.shape[2:])
        return jnp.moveaxis(a, 3, 1)

    q, k, v, beta, g = chunks(q), chunks(k), chunks(v), chunks(beta), chunks(g)
    q = q * (DK ** -0.5)
    gc = jnp.cumsum(g, axis=-1)
    causal = jnp.tril(jnp.ones((CHUNK, CHUNK), dtype=bool))
    strict = jnp.tril(jnp.ones((CHUNK, CHUNK), dtype=bool), -1)
    decay = jnp.exp(jnp.where(causal, gc[..., :, None] - gc[..., None, :], -jnp.inf))
    k_beta = k * beta[..., None]
    a_mat = jnp.where(strict, jnp.einsum('bhncd,bhnsd->bhncs', k_beta, k) * decay, 0.0)
    rhs = jnp.concatenate([v * beta[..., None], k_beta * jnp.exp(gc)[..., None]], axis=-1)
    sol = lax.linalg.triangular_solve(a_mat, rhs, left_side=True, lower=True, unit_diagonal=True)
    u, w = sol[..., :DV], sol[..., DV:]
    attn = jnp.einsum('bhncd,bhnsd->bhncs', q, k) * decay
    q_dec = q * jnp.exp(gc)[..., None]
    k_tail = k * jnp.exp(gc[..., -1:] - gc)[..., None]
    chunk_decay = jnp.exp(gc[..., -1])

    def step(state, inp):
        u_i, w_i, a_i, qd_i, kt_i, dec_i = inp
        v_new = u_i - jnp.einsum('bhcd,bhde->bhce', w_i, state)
        o_i = jnp.einsum('bhcd,bhde->bhce', qd_i, state) + jnp.einsum('bhcs,bhse->bhce', a_i, v_new)
        state = state * dec_i[..., None, None] + jnp.einsum('bhcd,bhce->bhde', kt_i, v_new)
        return state, o_i

    xs = tuple(jnp.moveaxis(a, 2, 0) for a in (u, w, attn, q_dec, k_tail, chunk_decay))
    s0 = jnp.zeros((b, H_B, DK, DV), f32)
    _, o = lax.scan(step, s0, xs)
    o = jnp.moveaxis(o, 0, 2)
    return jnp.moveaxis(o, 1, 3).reshape(b, t, H_B, DV)


def setup_inputs(seed: int = 0) -> dict:
    key = jax.random.key(seed)
    ks = jax.random.split(key, 13)
    f32 = jnp.float32
    x = jax.random.normal(ks[0], (BATCH, SEQ, D_MODEL), f32)
    pre_norm_g = 1.0 + 0.02 * jax.random.normal(ks[1], (DEPTH, D_MODEL), f32)
    w_in = jax.random.normal(ks[2], (DEPTH, D_MODEL, N_IN), f32) * (D_MODEL ** -0.5)
    sinks = jax.random.normal(ks[3], (DEPTH, H_A), f32)
    conv_w = jax.random.normal(ks[4], (DEPTH, CONV_K, C_CONV), f32) * (CONV_K ** -0.5)
    a_log = jnp.log(jax.random.uniform(ks[5], (DEPTH, H_B), f32, 1.0, 16.0))
    dt = jnp.exp(jax.random.uniform(ks[6], (DEPTH, H_B), f32, math.log(1e-3), math.log(1e-1)))
    dt_bias = dt + jnp.log(-jnp.expm1(-dt))
    gdn_norm_g = 1.0 + 0.02 * jax.random.normal(ks[7], (DEPTH, DV), f32)
    w_pa = jax.random.normal(ks[8], (DEPTH, WIDTH_A, D_MODEL), f32) * (WIDTH_A ** -0.5)
    w_pb = jax.random.normal(ks[9], (DEPTH, WIDTH_B, D_MODEL), f32) * (WIDTH_B ** -0.5)
    w_o = jax.random.normal(ks[10], (DEPTH, D_MODEL, D_MODEL), f32) * (D_MODEL ** -0.5)
    post_norm_g = 1.0 + 0.02 * jax.random.normal(ks[11], (DEPTH, D_MODEL), f32)
    return {'x': x, 'pre_norm_g': pre_norm_g, 'w_in': w_in, 'sinks': sinks, 'conv_w': conv_w,
            'a_log': a_log, 'dt_bias': dt_bias, 'gdn_norm_g': gdn_norm_g, 'w_pa': w_pa,
            'w_pb': w_pb, 'w_o': w_o, 'post_norm_g': post_norm_g}


def reference(x, pre_norm_g, w_in, sinks, conv_w, a_log, dt_bias, gdn_norm_g, w_pa, w_pb, w_o, post_norm_g):
    b, t, _ = x.shape
    offsets = np.cumsum(SPLIT_SIZES)[:-1].tolist()
    for layer in range(DEPTH):
        h = rms_norm(x, pre_norm_g[layer])
        proj = jnp.einsum('btd,dn->btn', h, w_in[layer])
        (q_a, k_a, v_a, z_a, qkv_b, z_b, beta_raw, alpha_raw, gate_a, gate_b) = jnp.split(proj, offsets, axis=-1)

        o_a = sliding_window_attention(q_a.reshape(b, t, H_A, HD_A), k_a.reshape(b, t, H_KV, HD_A),
                                       v_a.reshape(b, t, H_KV, HD_A), sinks[layer])
        y_a = jnp.einsum('btc,cd->btd', o_a * jax.nn.silu(z_a), w_pa[layer])

        qkv_b = jax.nn.silu(causal_depthwise_conv(qkv_b, conv_w[layer]))
        q_b, k_b, v_b = jnp.split(qkv_b, [H_B * DK, 2 * H_B * DK], axis=-1)
        q_b = l2_normalize(q_b.reshape(b, t, H_B, DK))
        k_b = l2_normalize(k_b.reshape(b, t, H_B, DK))
        v_b = v_b.reshape(b, t, H_B, DV)
        beta = jax.nn.sigmoid(beta_raw.astype(jnp.float32))
        g = -jnp.exp(a_log[layer].astype(jnp.float32)) * jax.nn.softplus(
            alpha_raw.astype(jnp.float32) + dt_bias[layer].astype(jnp.float32))
        o_b = gated_delta_rule(q_b, k_b, v_b, beta, g)
        o_b = rms_norm(o_b, gdn_norm_g[layer]).astype(x.dtype).reshape(b, t, WIDTH_B)
        y_b = jnp.einsum('btc,cd->btd', o_b * jax.nn.silu(z_b), w_pb[layer])

        merged = jax.nn.sigmoid(gate_a) * y_a + jax.nn.sigmoid(gate_b) * y_b
        out = jnp.einsum('btd,de->bte', merged, w_o[layer])
        x = x + rms_norm(out, post_norm_g[layer])
    return x
```

```python
import numpy as np
import concourse.bass as bass
import concourse.mybir as mybir
from concourse.bass_utils import run_bass_kernel_spmd

F32 = mybir.dt.float32
BF16 = mybir.dt.bfloat16
AF = mybir.ActivationFunctionType
ALU = mybir.AluOpType
AX = mybir.AxisListType

D = 4096
BATCH = 2
SEQ = 4096
DEPTH = 4
NCORES = 8
EPS = 1e-6
NEG = -30000.0
DEBUG = False
STOP_AFTER = None
ATTN_CUT = 0
ATTN_J = 0
GDN_CUT = 0


class _Cut(Exception):
    pass


class Buf:
    def __init__(self, t, name, multi=False):
        self.t = t
        self.name = name
        self.w = None
        self.ws = {}
        self.r = {}
        self.sem = None
        self.cnt = 0
        self.multi = multi
        self.key = id(self)

    def __getitem__(self, idx):
        return self.t[idx]


class LazyBuf(Buf):
    def __init__(self, mk, name):
        self._mk = mk
        self._t = None
        Buf.__init__(self, None, name, multi=True)

    @property
    def t(self):
        if self._t is None:
            self._t = self._mk()
        return self._t

    @t.setter
    def t(self, v):
        pass


class LazyIn:
    def __init__(self, mk):
        self._mk = mk
        self._h = None

    def ap(self):
        if self._h is None:
            self._h = self._mk()
        return self._h.ap()


class _XH:
    def __init__(self, lz):
        self.lz = lz

    def ap(self):
        return self.lz.ap()


class PView:
    def __init__(self, bank, lo, hi, name):
        self.b = bank
        self.t = bank.t
        self.lo, self.hi = lo, hi
        self.name = name
        self.multi = False
        self.key = bank.key
    w = property(lambda self: self.b.w, lambda self, v: setattr(self.b, 'w', v))
    ws = property(lambda self: self.b.ws, lambda self, v: setattr(self.b, 'ws', v))
    r = property(lambda self: self.b.r, lambda self, v: setattr(self.b, 'r', v))


class K:
    ENG = ('pe', 'act', 'dve', 'pool', 'sp')

    def __init__(self, nc):
        self.nc = nc
        self.q = {e: [] for e in self.ENG}
        self.esem = {e: nc.alloc_semaphore("es_" + e) for e in self.ENG}
        self.ecnt = {e: 0 for e in self.ENG}
        self.waited = {e: {} for e in self.ENG}
        self.pending = {e: [] for e in self.ENG}
        self.sempool = []
        self.live = []
        self.uid = 0
        self.ccsem = nc.alloc_semaphore("ccsem")
        self.cccnt = 0
        self.semobjs = {}

    def sb(self, name, shape, dtype, off):
        self.uid += 1
        t = self.nc.alloc_sbuf_tensor_at(f"{name}_{self.uid}", list(shape), dtype, offset=off)
        b = Buf(t, name)
        self.live.append(b)
        return b

    def dram(self, name, shape, dtype, kind="Internal"):
        t = self.nc.dram_tensor(name, list(shape), dtype, kind=kind)
        b = Buf(t, name, multi=True)
        return b

    def getsem(self, b):
        if b.sem is None:
            if self.sempool:
                b.sem, b.cnt = self.sempool.pop()
            else:
                self.uid += 1
                b.sem = self.nc.alloc_semaphore(f"bs_{self.uid}")
                b.cnt = 0
        return b.sem

    def release(self, bufs):
        for b in bufs:
            if b.sem is not None:
                self.sempool.append((b.sem, b.cnt))
                b.sem = None
            if b in self.live:
                self.live.remove(b)

    def _need(self, e, reads, writes):
        need = {}

        def add(tok):
            if tok is None:
                return
            s, v = tok
            if need.get(s, 0) < v:
                need[s] = v
        for b in reads:
            add(b.w)
            for s, v in b.ws.items():
                add((s, v))
        for b in writes:
            if not b.multi:
                add(b.w)
            for s, v in b.r.items():
                add((s, v))
        own = self.esem[e]
        for s, v in need.items():
            if s is own and e == 'pe':
                continue
            if self.waited[e].get(s, 0) >= v:
                continue
            self.waited[e][s] = v
            self.q[e].append(lambda eng, s=s, v=v: eng.wait_ge(s, v))

    def _commit(self, tok, reads, writes):
        for b in writes:
            if b.multi:
                if b.r:
                    b.ws = {}
                    b.w = None
                b.r = {}
                if b.ws.get(tok[0], 0) < tok[1]:
                    b.ws[tok[0]] = tok[1]
            else:
                b.w = tok
                b.r = {}
        for b in reads:
            if b.r.get(tok[0], 0) < tok[1]:
                b.r[tok[0]] = tok[1]

    def op(self, e, fn, reads=(), writes=(), signal=True):
        for pe_ in self.ENG:
            if pe_ != e and self.pending[pe_]:
                pb = set()
                for rs, ws in self.pending[pe_]:
                    pb.update(x.key for x in rs)
                    pb.update(x.key for x in ws)
                for b in list(reads) + list(writes):
                    assert b.key not in pb, f"buffer {b.name} has unsignaled op pending on {pe_}"
        self._need(e, reads, writes)
        if signal:
            self.ecnt[e] += 1
            tok = (self.esem[e], self.ecnt[e])
            self.q[e].append(lambda eng, fn=fn, s=tok[0]: fn(eng).then_inc(s, 1))
            for rs, ws in self.pending[e]:
                self._commit(tok, rs, ws)
            self.pending[e] = []
            self._commit(tok, reads, writes)
        else:
            self.q[e].append(lambda eng, fn=fn: fn(eng))
            self.pending[e].append((list(reads), list(writes)))

    def dma(self, e, pairs, reads, writes, sembuf):
        self._need(e, reads, writes)
        s = self.getsem(sembuf)
        for (o, i) in pairs:
            sembuf.cnt += 16
            self.q[e].append(lambda eng, o=o, i=i, s=s: eng.dma_start(out=o, in_=i).then_inc(s, 16))
        tok = (s, sembuf.cnt)
        self._commit(tok, reads, writes)

    def allgather(self, src, dst):
        if getattr(self, 'no_cc', False):
            return
        self._need('pool', [src], [dst])
        self.cccnt += 1
        s = self.ccsem
        v = self.cccnt
        sa, da = src.t.ap().opt(), dst.t.ap().opt()
        self.q['pool'].append(lambda eng, sa=sa, da=da, s=s: eng.collective_compute(
            "AllGather", ALU.bypass, replica_groups=[list(range(NCORES))], ins=[sa], outs=[da]).then_inc(s))
        self.q['pool'].append(lambda eng, s=s, v=v: eng.wait_ge(s, v))
        self.waited['pool'][s] = v
        self._commit((s, v), [src], [dst])

    def barrier(self):
        toks = {}
        for e in self.ENG:
            assert not self.pending[e]
            if self.ecnt[e]:
                toks[self.esem[e]] = self.ecnt[e]
        for b in self.live:
            if b.sem is not None and b.cnt:
                toks[b.sem] = b.cnt
        for s, c in self.sempool:
            if c:
                toks[s] = max(toks.get(s, 0), c)
        for e in self.ENG:
            for s, v in toks.items():
                if s is self.esem[e]:
                    continue
                if self.waited[e].get(s, 0) >= v:
                    continue
                self.waited[e][s] = v
                self.q[e].append(lambda eng, s=s, v=v: eng.wait_ge(s, v))

    def finish_waits(self, e, bufs):
        self._need(e, [], bufs)
        self._need(e, bufs, [])


def build(plan=None):
    NTOK = BATCH * SEQ
    NT128 = NTOK // 128
    BPS = SEQ // 128
    TT = 512 if SEQ % 512 == 0 else SEQ
    SUB = TT // 128
    NTT = NTOK // TT
    KC = D // 128

    nc = bass.Bass("TRN2", target_bir_lowering=False)
    try:
        nc.allow_low_precision("bf16 matmul operands, fp32 accumulation")
    except Exception:
        pass
    k = K(nc)

    WD = DEPTH if plan is None else 1
    k.no_cc = plan is not None
    ext_outs = []

    def din(name, shape, dt=F32):
        return LazyIn(lambda: nc.dram_tensor(name, list(shape), dt, kind="ExternalInput"))

    x_in = din("x", [NTOK, 512])
    w2a = din("w2a", [WD, D, 704])
    w2b = din("w2b", [WD, D, 1028])
    wg = din("wg", [WD, D, 1024])
    wpa = din("wpa", [WD, 2048, 512])
    wpb = din("wpb", [WD, 2048, 512])
    wo = din("wo", [WD, D, 512])
    gpre = din("gpre", [WD, 128, 4])
    gpost = din("gpost", [WD, 128, 512])
    sinks = din("sinks", [WD, 128, 4])
    convw = din("convw", [WD, 128, 24])
    alog = din("alog", [WD, 128, 2])
    dtb = din("dtb", [WD, 128, 2])
    gng = din("gng", [WD, 128, 128])
    cst = din("cst", [128, 5, 128])
    abias = din("abias", [128, 2, 4, 128])

    okind = "ExternalOutput"

    def ddram(name, shape, dtype, kind=None):
        if kind is None:
            if plan is not None and name in plan["ins"]:
                kind = "ExternalInput"
            elif plan is not None and name in plan["outs"]:
                kind = okind
            else:
                kind = "Internal"
        b = LazyBuf(lambda: nc.dram_tensor(name, list(shape), dtype, kind=kind), name)
        if kind == okind:
            ext_outs.append(b)
        return b

    y_out = ddram("y", [NTOK, 512], F32, kind=okind) if plan is None or "y" in plan["outs"] else None
    hT_part = ddram("hT_part", [512, NTOK], BF16)
    hT_full = ddram("hT_full", [D, NTOK], BF16)
    g_part = ddram("g_part", [512, NTOK], BF16)
    g_full = ddram("g_full", [D, NTOK], BF16)
    m_part = ddram("m_part", [512, NTOK], BF16)
    m_full = ddram("m_full", [D, NTOK], BF16)
    o_part = ddram("o_part", [NTOK, 512], F32)
    xs = ddram("xs", [NTOK, 512], F32)
    ssq_part = ddram("ssq_part", [128, NT128], F32)
    ssq_full = ddram("ssq_full", [NCORES * 128, NT128], F32)
    dbg = {}
    if DEBUG and plan is None:
        for nm, shp, dt in (("d_hT", [D, NTOK], BF16), ("d_g", [D, NTOK], BF16),
                            ("d_m", [D, NTOK], BF16), ("d_o", [NTOK, 512], F32)):
            dbg[nm] = ddram(nm, shp, dt, kind=okind)
    x_src = LazyBuf(lambda: None, "x_in")
    x_src._mk = lambda: _XH(x_in)

    psb = []
    for i in range(8):
        t = nc.alloc_psum_tensor(f"ps{i}", [128, 512], F32)
        psb.append(t)

    class PQ:
        pass

    PSB = [Buf(psb[i], f"psbank{i}") for i in range(8)]

    def psq(bank, lo, hi, name):
        return PView(PSB[bank], lo, hi, name)

    def P(b, *idx):
        return b.t[:, b.lo:b.hi]

    off = 16512

    def alloc(name, shape, dtype):
        nonlocal off
        nbytes = int(np.prod(shape[1:])) * (4 if dtype == F32 else 2)
        nbytes = (nbytes + 31) // 32 * 32
        b = k.sb(name, shape, dtype, off)
        off += nbytes
        assert off <= 229376, (name, off)
        return b

    CST = alloc("cst", [128, 5, 128], F32)
    ABI = alloc("abias", [128, 2, 4, 128], F32)
    IDB = alloc("idb", [128, 128], BF16)
    EPSC = alloc("epsc", [128, 2], F32)
    k.op('dve', lambda e: e.memset(EPSC[:, 0:1], EPS), [], [EPSC])
    k.op('dve', lambda e: e.memset(EPSC[:, 1:2], 1.0), [], [EPSC])
    k.dma('sp', [(CST[:], cst.ap())], [], [CST], CST)
    k.dma('sp', [(ABI[:], abias.ap())], [], [ABI], ABI)
    k.op('dve', lambda e: e.tensor_copy(IDB[:], CST[:, 0, :]), [CST], [IDB])
    I_ = lambda: CST[:, 0, :]
    U_ = lambda: CST[:, 1, :]
    LS_ = lambda: CST[:, 2, :]
    ONES_ = lambda: CST[:, 3, :]
    MB_ = lambda: CST[:, 4, :]
    base_off = off

    def load_w(dst, src_ap, nk, ncols, tag):
        stg = [alloc(f"{tag}_stg{i}", [128, 2, ncols], F32) for i in range(2)]
        v = src_ap.rearrange("(kc p) n -> p kc n", p=128)
        for i, k0 in enumerate(range(0, nk, 2)):
            sb_ = stg[i % 2]
            k.dma('sp', [(sb_[:], v[:, k0:k0 + 2, :])], [], [sb_], sb_)
            k.op('act', lambda e, sb_=sb_, k0=k0: e.activation(dst[:, k0:k0 + 2, :], sb_[:], AF.Copy), [sb_], [dst])

    def load_hT(dst, src, t0, nt):
        v = src.t.ap()[:, t0:t0 + nt].rearrange("(kc p) t -> p kc t", p=128)
        pairs = [(dst[:, k0:k0 + 8, :], v[:, k0:k0 + 8, :]) for k0 in range(0, KC, 8)]
        k.dma('sp', pairs, [src], [dst], dst)

    def phase_ssq(src_buf, l):
        nonlocal off
        off = base_off
        XT = [alloc(f"sq_x{i}", [128, 512], F32) for i in range(2)]
        JK = alloc("sq_junk", [128, 512], F32)
        SS = alloc("sq_ss", [128, NT128], F32)
        k.op('dve', lambda e: e.memset(SS[:], 0.0), [], [SS])
        for i in range(NT128):
            xb = XT[i % 2]
            k.dma('sp', [(xb[:], src_buf.t.ap()[i * 128:(i + 1) * 128, :])], [src_buf], [xb], xb)
            k.op('act', lambda e, xb=xb, i=i: e.activation(JK[:], xb[:], AF.Square, accum_out=SS[:, i:i + 1]),
                 [xb], [JK, SS])
        k.dma('sp', [(ssq_part.t.ap(), SS[:])], [SS], [ssq_part], SS)
        k.allgather(ssq_part, ssq_full)
        k.barrier()
        k.release(XT + [JK, SS])

    def load_rstd(R8, RS):
        k.dma('sp', [(R8[:], ssq_full.t.ap().rearrange("(r p) n -> p r n", p=128))], [ssq_full], [R8], R8)
        k.op('dve', lambda e: e.reduce_sum(RS[:], R8[:].rearrange("p r n -> p n r"), axis=AX.X), [R8], [RS])
        k.op('act', lambda e: e.activation(RS[:], RS[:], AF.Sqrt, bias=EPSC[:, 0:1], scale=1.0 / D), [RS, EPSC], [RS])
        k.op('dve', lambda e: e.reciprocal(RS[:], RS[:]), [RS], [RS])

    def phase_hT(src_buf, l):
        nonlocal off
        off = base_off
        R8 = alloc("h_r8", [128, NCORES, NT128], F32)
        RS = alloc("h_rs", [128, NT128], F32)
        GP = alloc("h_gp", [128, 4], F32)
        XT = [alloc(f"h_x{i}", [128, 512], F32) for i in range(2)]
        XS = [alloc(f"h_xs{i}", [128, 512], F32) for i in range(2)]
        HT = [alloc(f"h_ht{i}", [128, 4, 512], BF16) for i in range(2)]
        pst = [psq(0, 0, 512, "h_ps0"), psq(1, 0, 512, "h_ps1")]
        load_rstd(R8, RS)
        k.dma('sp', [(GP[:], gpre.ap()[l])], [], [GP], GP)
        for i in range(NT128):
            xb, xsb, ps = XT[i % 2], XS[i % 2], pst[i % 2]
            hb = HT[(i // 4) % 2]
            k.dma('sp', [(xb[:], src_buf.t.ap()[i * 128:(i + 1) * 128, :])], [src_buf], [xb], xb)
            k.op('dve', lambda e, xb=xb, xsb=xsb, i=i: e.tensor_scalar(xsb[:], xb[:], RS[:, i:i + 1], None, ALU.mult),
                 [xb, RS], [xsb])
            for m in range(4):
                k.op('pe', lambda e, ps=ps, xsb=xsb, m=m: e.transpose(ps.t[:, m * 128:(m + 1) * 128], xsb[:, m * 128:(m + 1) * 128], I_()),
                     [xsb, CST], [ps], signal=(m == 3))
            for m in range(4):
                k.op('act', lambda e, ps=ps, hb=hb, m=m, i=i: e.activation(
                    hb[:, m, (i % 4) * 128:(i % 4 + 1) * 128], ps.t[:, m * 128:(m + 1) * 128], AF.Copy, scale=GP[:, m:m + 1]),
                    [ps, GP], [hb])
            if i % 4 == 3:
                t0 = (i // 4) * 512
                k.dma('sp', [(hT_part.t.ap()[:, t0:t0 + 512].rearrange("(m p) t -> p m t", p=128), hb[:])],
                      [hb], [hT_part], hb)
        k.allgather(hT_part, hT_full)
        k.barrier()
        k.release([R8, RS, GP] + XT + XS + HT)

    def phase_attn(l):
        nonlocal off
        off = base_off
        W = alloc("a_w", [128, KC, 704], BF16)
        HTb = [alloc(f"a_ht{i}", [128, KC, TT], BF16) for i in range(2)]
        KT = alloc("a_kt", [128, NTOK], BF16)
        VA = alloc("a_v", [128, NT128, 128], BF16)
        QT = alloc("a_qt", [128, 4, TT], BF16)
        SZ = alloc("a_sz", [128, 2, TT], F32)
        PM = [alloc(f"a_pm{i}", [128, 4, 128], F32) for i in range(2)]
        PX = [alloc(f"a_px{i}", [128, 4, 128], BF16) for i in range(2)]
        SK = alloc("a_sk", [128, 4], F32)
        VT = alloc("a_vt", [128, TT], F32)
        DEN = alloc("a_den", [128, 4], F32)
        OA = alloc("a_oa", [128, 256], F32)
        GA = [alloc(f"a_ga{i}", [128, 2, TT], BF16) for i in range(2)]
        pp = [psq(0, 0, 512, "a_pp0"), psq(1, 0, 512, "a_pp1")]
        pv = psq(2, 0, 512, "a_pv")
        pst = [psq(3, 0, 512, "a_st0"), psq(4, 0, 512, "a_st1")]
        po = psq(5, 0, 512, "a_po")
        ptr = psq(6, 0, 512, "a_ptr")

        def fin():
            k.barrier()
            k.release([b for b in list(k.live) if b.name.startswith('a_')])

        load_w(W, w2a.ap()[l], KC, 704, 'a')
        if ATTN_CUT == 1:
            return fin()
        k.dma('sp', [(SK[:], sinks.ap()[l])], [], [SK], SK)
        k.op('act', lambda e: e.activation(SK[:], SK[:], AF.Exp), [SK], [SK])
        k.op('dve', lambda e: e.memset(VA[:], 1.0), [], [VA])
        k.op('dve', lambda e: e.memset(QT[:], 0.0), [], [QT])
        ci = 0
        for j in range(NTT):
            hb = HTb[j % 2]
            load_hT(hb, hT_full, j * TT, TT)
            gab = GA[j % 2]
            if ATTN_CUT == 2 and j == ATTN_J:
                return fin()
            for m in range(5):
                ps = pp[ci % 2]
                ci += 1
                for kc in range(KC):
                    k.op('pe', lambda e, ps=ps, m=m, kc=kc, hb=hb: e.matmul(
                        ps.t[:, 0:TT], W[:, kc, m * 128:(m + 1) * 128], hb[:, kc, :], start=(kc == 0), stop=(kc == KC - 1)),
                        [W, hb], [ps], signal=(kc == KC - 1))
                if m < 2:
                    k.op('dve', lambda e, ps=ps, m=m: e.tensor_copy(QT[0:64, 2 * m, :], ps.t[0:64, 0:TT]), [ps], [QT])
                    k.op('dve', lambda e, ps=ps, m=m: e.tensor_copy(QT[64:128, 2 * m + 1, :], ps.t[64:128, 0:TT]), [ps], [QT])
                elif m == 2:
                    k.op('dve', lambda e, ps=ps, j=j: e.tensor_copy(KT[:, j * TT:(j + 1) * TT], ps.t[:, 0:TT]), [ps], [KT])
                else:
                    k.op('act', lambda e, ps=ps, m=m: e.activation(SZ[:, m - 3, :], ps.t[:, 0:TT], AF.Silu), [ps], [SZ])
            if ATTN_CUT == 3 and j == ATTN_J:
                return fin()
            ps = pp[ci % 2]
            ci += 1
            for kc in range(KC):
                k.op('pe', lambda e, ps=ps, kc=kc, hb=hb: e.matmul(
                    ps.t[:, 0:TT], W[:, kc, 576:704], hb[:, kc, :], start=(kc == 0), stop=(kc == KC - 1)),
                    [W, hb], [ps], signal=(kc == KC - 1))
            k.op('act', lambda e, ps=ps: e.activation(VT[:], ps.t[:, 0:TT], AF.Copy), [ps], [VT])
            for s in range(SUB):
                k.op('pe', lambda e, s=s: e.transpose(pv.t[:, s * 128:(s + 1) * 128], VT[:, s * 128:(s + 1) * 128], I_()),
                     [VT, CST], [pv], signal=(s == SUB - 1))
            k.op('act', lambda e, j=j: e.activation(
                VA[:, j * SUB:(j + 1) * SUB, 0:64], pv.t[:, 0:SUB * 128].rearrange("p (s d) -> p s d", d=128)[:, :, 64:128], AF.Copy),
                [pv], [VA])
            if ATTN_CUT in (4, 41) and j == ATTN_J:
                return fin()
            for s in range(SUB):
                nb = j * SUB + s
                first = (nb % BPS == 0)
                if ATTN_CUT == 5 and s == 1 and j == ATTN_J:
                    return fin()
                pmb, pxb = PM, PX
                for h in range(4):
                    pb_ = 64 * (h % 2)
                    ch = h // 2
                    k.op('pe', lambda e, h=h, pb_=pb_, ch=ch, nb=nb, s=s: e.matmul(
                        pst[0].t[:, h * 128:(h + 1) * 128], KT[:, nb * 128:(nb + 1) * 128],
                        QT[:, h, s * 128:(s + 1) * 128], start=True, stop=True),
                        [KT, QT], [pst[0]], signal=(h == 3))
                if not first:
                    for h in range(4):
                        pb_ = 64 * (h % 2)
                        ch = h // 2
                        k.op('pe', lambda e, h=h, pb_=pb_, ch=ch, nb=nb, s=s: e.matmul(
                            pst[1].t[:, h * 128:(h + 1) * 128], KT[:, (nb - 1) * 128:nb * 128],
                            QT[:, h, s * 128:(s + 1) * 128], start=True, stop=True),
                            [KT, QT], [pst[1]], signal=(h == 3))
                nh = 1 if first else 2
                for a in range(nh):
                    k.op('dve', lambda e, a=a: e.scalar_tensor_tensor(
                        PM[a][:].rearrange("p h q -> p (h q)"), pst[a].t[:, 0:512], 0.125,
                        ABI[:, a, :, :].rearrange("p h q -> p (h q)"), ALU.mult, ALU.add), [pst[a], ABI], [PM[a]])
                    k.op('act', lambda e, a=a: e.activation(PX[a][:], PM[a][:], AF.Exp), [PM[a]], [PX[a]])
                for h in range(4):
                    k.op('pe', lambda e, h=h, nb=nb, first=first: e.matmul(
                        po.t[:, h * 128:(h + 1) * 128], PX[0][:, h, :], VA[:, nb, :], start=True, stop=first),
                        [PX[0], VA], [po], signal=(first and h == 3))
                    if not first:
                        k.op('pe', lambda e, h=h, nb=nb: e.matmul(
                            po.t[:, h * 128:(h + 1) * 128], PX[1][:, h, :], VA[:, nb - 1, :], start=False, stop=True),
                            [PX[1], VA], [po], signal=(h == 3))
                pov = lambda: po.t[:, 0:512].rearrange("p (h d) -> p h d", d=128)
                k.op('dve', lambda e: e.tensor_tensor(DEN[:], pov()[:, :, 64], SK[:], ALU.add), [po, SK], [DEN])
                k.op('dve', lambda e: e.reciprocal(DEN[:], DEN[:]), [DEN], [DEN])
                for h in range(4):
                    k.op('act', lambda e, h=h: e.activation(
                        OA[:, h * 64:(h + 1) * 64], po.t[:, h * 128:h * 128 + 64], AF.Copy, scale=DEN[:, h:h + 1]),
                        [po, DEN], [OA])
                for i2 in range(2):
                    k.op('pe', lambda e, i2=i2: e.transpose(ptr.t[:, i2 * 128:(i2 + 1) * 128], OA[:, i2 * 128:(i2 + 1) * 128], I_()),
                         [OA, CST], [ptr], signal=(i2 == 1))
                k.op('dve', lambda e, s=s, gab=gab: e.tensor_tensor(
                    gab[:, :, s * 128:(s + 1) * 128], ptr.t[:, 0:256].rearrange("p (i q) -> p i q", q=128),
                    SZ[:, :, s * 128:(s + 1) * 128], ALU.mult), [ptr, SZ], [gab])
            if ATTN_CUT == 6 and j == ATTN_J:
                return fin()
            k.dma('sp', [(g_part.t.ap()[0:256, j * TT:(j + 1) * TT].rearrange("(m p) t -> p m t", p=128), gab[:])],
                  [gab], [g_part], gab)
            if ATTN_CUT == 7 and j == ATTN_J:
                return fin()
        fin()

    def phase_gdn(l):
        try:
            phase_gdn_body(l)
        except _Cut:
            for e_ in k.ENG:
                k.pending[e_] = []
            k.dma('sp', [(g_part.t.ap()[0:128, 0:128], IDB[:])], [IDB], [g_part], IDB)
        k.barrier()
        k.release([b for b in list(k.live) if b.name.startswith("b_")])

    def gcut(n):
        if GDN_CUT == n:
            raise _Cut()

    def phase_gdn_body(l):
        nonlocal off
        off = base_off
        W = alloc("b_w", [128, KC, 1028], BF16)
        HTb = [alloc(f"b_ht{i}", [128, KC, TT], BF16) for i in range(2)]
        XR = alloc("b_xr", [128, 6, TT + 4], F32)
        ACC = alloc("b_acc", [128, TT], F32)
        Y = alloc("b_y", [128, 6, TT], F32)
        SQ = alloc("b_sq", [128, TT], F32)
        BA = alloc("b_ba", [128, TT], F32)
        RN = alloc("b_rn", [128, TT], F32)
        SZ = alloc("b_sz", [128, 2, TT], F32)
        CW = alloc("b_cw", [128, 24], F32)
        AL = alloc("b_al", [128, 2], F32)
        DT = alloc("b_dt", [128, 2], F32)
        GN = alloc("b_gn", [128, 128], F32)
        SMr = alloc("b_smr", [128, SUB, 4], F32)
        BE = alloc("b_be", [128, SUB, 2], F32)
        NBE = alloc("b_nbe", [128, SUB, 2], F32)
        GG = alloc("b_gg", [128, SUB, 2], F32)
        NGG = alloc("b_ngg", [128, SUB, 2], F32)
        T1 = alloc("b_t1", [128, SUB, 2], F32)
        GBt = alloc("b_gb", [128, 128], F32)
        NGBt = alloc("b_ngb", [128, 128], F32)
        E = alloc("b_e", [128, 128], F32)
        EI = alloc("b_ei", [128, 128], F32)
        EGB = alloc("b_egb", [128, 128], F32)
        EV = alloc("b_ev", [128, 4], F32)
        BG = alloc("b_bg", [128, 1], F32)
        NA = [alloc(f"b_na{i}", [128, 128], F32) for i in range(2)]
        NB = [alloc(f"b_nb{i}", [128, 128], F32) for i in range(2)]
        PP = [alloc(f"b_pp{i}", [128, 128], F32) for i in range(2)]
        ATT = alloc("b_att", [128, 128], F32)
        ATTt = alloc("b_attt", [128, 128], F32)
        KTG = alloc("b_ktg", [128, 128], F32)
        KTL = alloc("b_ktl", [128, 128], F32)
        VB = alloc("b_vb", [128, 128], F32)
        QD = alloc("b_qd", [128, 128], F32)
        UU = alloc("b_u", [128, 128], F32)
        WT = alloc("b_wt", [128, 128], F32)
        VN = alloc("b_vn", [128, 128], F32)
        ST = [alloc(f"b_s{i}", [128, 128], F32) for i in range(2)]
        ON = alloc("b_on", [128, 128], F32)
        OS = alloc("b_os", [128, 4], F32)
        JK = alloc("b_jk", [128, 128], F32)
        GBo = [alloc(f"b_go{i}", [128, 2, TT], BF16) for i in range(2)]
        pp = [psq(0, 0, 512, "b_pp0"), psq(1, 0, 512, "b_pp1")]
        pss = psq(2, 0, 512, "b_pss")
        pvec = psq(3, 0, 64, "b_pvec")
        psm = psq(3, 64, 128, "b_psm")
        q_ = {}
        q_["u"] = psq(3, 128, 256, "b_u")
        q_["po"] = psq(3, 384, 512, "b_po")
        q_["onT"] = psq(2, 0, 128, "b_onT")
        for i, nm in enumerate(["D", "gcb", "KK", "QK"]):
            q_[nm] = psq(4, i * 128, (i + 1) * 128, "b_" + nm)
        for i, nm in enumerate(["kt", "vt"]):
            q_[nm] = psq(5, i * 128, (i + 1) * 128, "b_" + nm)
        for i, nm in enumerate(["Na", "Nb", "trA", "trB"]):
            q_[nm] = psq(6, i * 128, (i + 1) * 128, "b_" + nm)
        for i, nm in enumerate(["Pc", "wT", "p1"]):
            q_[nm] = psq(7, i * 128, (i + 1) * 128, "b_" + nm)
        ps_s = psq(7, 384, 512, "b_pss2")

        load_w(W, w2b.ap()[l], KC, 1028, 'b')
        k.dma('sp', [(CW[:], convw.ap()[l])], [], [CW], CW)
        k.dma('sp', [(AL[:], alog.ap()[l])], [], [AL], AL)
        k.dma('sp', [(DT[:], dtb.ap()[l])], [], [DT], DT)
        k.dma('sp', [(GN[:], gng.ap()[l])], [], [GN], GN)
        k.op('act', lambda e: e.activation(AL[:], AL[:], AF.Exp), [AL], [AL])
        gcut(1)
        ci = 0
        sidx = {}
        for j in range(NTT):
            hb = HTb[j % 2]
            load_hT(hb, hT_full, j * TT, TT)
            gob = GBo[j % 2]
            seq_start = ((j * TT) % SEQ == 0)
            if seq_start:
                k.op('dve', lambda e: e.memset(XR[:, :, 0:4], 0.0), [], [XR])
            else:
                k.op('dve', lambda e: e.tensor_copy(XR[:, :, 1:4], XR[:, :, TT + 1:TT + 4]), [XR], [XR])
            for m in range(8):
                ps = pp[ci % 2]
                ci += 1
                for kc in range(KC):
                    k.op('pe', lambda e, ps=ps, m=m, kc=kc, hb=hb: e.matmul(
                        ps.t[:, 0:TT], W[:, kc, m * 128:(m + 1) * 128], hb[:, kc, :], start=(kc == 0), stop=(kc == KC - 1)),
                        [W, hb], [ps], signal=(kc == KC - 1))
                if m < 6:
                    k.op('act', lambda e, ps=ps, m=m: e.activation(XR[:, m, 4:TT + 4], ps.t[:, 0:TT], AF.Copy), [ps], [XR])
                    k.op('dve', lambda e, m=m: e.tensor_scalar(ACC[:], XR[:, m, 1:TT + 1], CW[:, m * 4:m * 4 + 1], None, ALU.mult),
                         [XR, CW], [ACC])
                    for jj in range(1, 4):
                        k.op('dve', lambda e, m=m, jj=jj: e.scalar_tensor_tensor(
                            ACC[:], XR[:, m, 1 + jj:TT + 1 + jj], CW[:, m * 4 + jj:m * 4 + jj + 1], ACC[:], ALU.mult, ALU.add),
                            [XR, CW, ACC], [ACC])
                    k.op('act', lambda e, m=m: e.activation(Y[:, m, :], ACC[:], AF.Silu), [ACC], [Y])
                    if m < 4:
                        k.op('dve', lambda e, m=m: e.tensor_tensor(SQ[:], Y[:, m, :], Y[:, m, :], ALU.mult), [Y], [SQ])
                        k.op('pe', lambda e: e.matmul(pss.t[:, 0:TT], ONES_(), SQ[:], start=True, stop=True), [CST, SQ], [pss])
                        k.op('act', lambda e: e.activation(RN[:], pss.t[:, 0:TT], AF.Sqrt, bias=EPSC[:, 0:1]), [pss, EPSC], [RN])
                        k.op('dve', lambda e: e.reciprocal(RN[:], RN[:]), [RN], [RN])
                        sc = (128.0 ** -0.5) if m < 2 else 1.0
                        k.op('dve', lambda e, m=m, sc=sc: e.scalar_tensor_tensor(
                            Y[:, m, :], Y[:, m, :], sc, RN[:], ALU.mult, ALU.mult), [Y, RN], [Y])
                else:
                    k.op('act', lambda e, ps=ps, m=m: e.activation(SZ[:, m - 6, :], ps.t[:, 0:TT], AF.Silu), [ps], [SZ])
            gcut(2)
            ps = pp[ci % 2]
            ci += 1
            for kc in range(KC):
                k.op('pe', lambda e, ps=ps, kc=kc, hb=hb: e.matmul(
                    ps.t[:, 0:TT], W[:, kc, 900:1028], hb[:, kc, :], start=(kc == 0), stop=(kc == KC - 1)),
                    [W, hb], [ps], signal=(kc == KC - 1))
            k.op('act', lambda e, ps=ps: e.activation(BA[:], ps.t[:, 0:TT], AF.Copy), [ps], [BA])
            for s in range(SUB):
                k.op('pe', lambda e, s=s: e.transpose(pss.t[:, s * 128:(s + 1) * 128], BA[:, s * 128:(s + 1) * 128], I_()),
                     [BA, CST], [pss], signal=(s == SUB - 1))
            k.op('act', lambda e: e.activation(
                SMr[:], pss.t[:, 0:SUB * 128].rearrange("p (s c) -> p s c", c=128)[:, :, 124:128], AF.Copy), [pss], [SMr])
            k.op('act', lambda e: e.activation(BE[:], SMr[:, :, 0:2], AF.Sigmoid), [SMr], [BE])
            k.op('dve', lambda e: e.tensor_scalar(NBE[:], BE[:], -1.0, None, ALU.mult), [BE], [NBE])
            for s in range(SUB):
                k.op('dve', lambda e, s=s: e.tensor_tensor(T1[:, s, :], SMr[:, s, 2:4], DT[:], ALU.add), [SMr, DT], [T1])
            k.op('act', lambda e: e.activation(T1[:], T1[:], AF.Exp), [T1], [T1])
            k.op('act', lambda e: e.activation(T1[:], T1[:], AF.Ln, bias=EPSC[:, 1:2]), [T1, EPSC], [T1])
            for s in range(SUB):
                k.op('dve', lambda e, s=s: e.tensor_tensor(NGG[:, s, :], T1[:, s, :], AL[:], ALU.mult), [T1, AL], [NGG])
            k.op('dve', lambda e: e.tensor_scalar(GG[:], NGG[:], -1.0, None, ALU.mult), [NGG], [GG])

            gcut(3)
            for s in range(SUB):
                tok0 = s * 128
                for hh in range(2):
                    qT = lambda hh=hh, tok0=tok0: Y[:, hh, tok0:tok0 + 128]
                    kT = lambda hh=hh, tok0=tok0: Y[:, 2 + hh, tok0:tok0 + 128]
                    vT = lambda hh=hh, tok0=tok0: Y[:, 4 + hh, tok0:tok0 + 128]
                    gcol = lambda hh=hh, s=s: GG[:, s, hh:hh + 1]
                    ngcol = lambda hh=hh, s=s: NGG[:, s, hh:hh + 1]
                    bcol = lambda hh=hh, s=s: BE[:, s, hh:hh + 1]
                    nbcol = lambda hh=hh, s=s: NBE[:, s, hh:hh + 1]
                    Q = q_
                    k.op('dve', lambda e, gcol=gcol: e.tensor_scalar(GBt[:], ONES_(), gcol(), None, ALU.mult), [CST, GG], [GBt])
                    k.op('dve', lambda e, ngcol=ngcol: e.tensor_scalar(NGBt[:], ONES_(), ngcol(), None, ALU.mult), [CST, NGG], [NGBt])
                    k.op('pe', lambda e: e.matmul(P(Q["D"]), U_(), GBt[:], start=True, stop=False), [CST, GBt], [Q["D"]], signal=False)
                    k.op('pe', lambda e: e.matmul(P(Q["D"]), NGBt[:], U_(), start=False, stop=False), [CST, NGBt], [Q["D"]], signal=False)
                    k.op('pe', lambda e: e.matmul(P(Q["D"]), I_(), MB_(), start=False, stop=True), [CST], [Q["D"]])
                    k.op('pe', lambda e: e.matmul(P(Q["gcb"]), GBt[:], U_(), start=True, stop=True), [CST, GBt], [Q["gcb"]])
                    k.op('pe', lambda e, gcol=gcol: e.matmul(pvec.t[:, 0:1], U_(), gcol(), start=True, stop=True), [CST, GG], [pvec], signal=False)
                    k.op('pe', lambda e, gcol=gcol: e.matmul(pvec.t[:, 1:2], LS_(), gcol(), start=True, stop=True), [CST, GG], [pvec], signal=False)
                    k.op('pe', lambda e, gcol=gcol: e.matmul(pvec.t[:, 2:3], ONES_(), gcol(), start=True, stop=True), [CST, GG], [pvec])
                    k.op('pe', lambda e, kT=kT: e.matmul(P(Q["KK"]), kT(), kT(), start=True, stop=True), [Y], [Q["KK"]])
                    k.op('pe', lambda e, kT=kT, qT=qT: e.matmul(P(Q["QK"]), qT(), kT(), start=True, stop=True), [Y], [Q["QK"]])
                    k.op('pe', lambda e, kT=kT: e.transpose(P(Q["kt"]), kT(), I_()), [Y, CST], [Q["kt"]])
                    k.op('pe', lambda e, vT=vT: e.transpose(P(Q["vt"]), vT(), I_()), [Y, CST], [Q["vt"]])
                    k.op('act', lambda e: e.activation(E[:], P(Q["D"]), AF.Exp), [Q["D"]], [E])
                    k.op('act', lambda e: e.activation(EGB[:], P(Q["gcb"]), AF.Exp), [Q["gcb"]], [EGB])
                    k.op('act', lambda e: e.activation(EV[:, 0:3], pvec.t[:, 0:3], AF.Exp), [pvec], [EV])
                    k.op('dve', lambda e, bcol=bcol: e.tensor_tensor(BG[:], EV[:, 0:1], bcol(), ALU.mult), [EV, BE], [BG])
                    gcut(4)
                    na, nb_, pq = NA[0], NB[0], PP[0]
                    k.op('dve', lambda e, nbcol=nbcol, na=na: e.scalar_tensor_tensor(
                        na[:], P(Q["KK"]), nbcol(), E[:], ALU.mult, ALU.mult), [Q["KK"], NBE, E], [na])
                    k.op('dve', lambda e: e.tensor_tensor(EI[:], E[:], I_(), ALU.add), [E, CST], [EI])
                    k.op('dve', lambda e: e.tensor_tensor(ATT[:], P(Q["QK"]), EI[:], ALU.mult), [Q["QK"], EI], [ATT])
                    k.op('act', lambda e: e.activation(KTG[:], P(Q["kt"]), AF.Copy, scale=BG[:, 0:1]), [Q["kt"], BG], [KTG])
                    k.op('act', lambda e: e.activation(KTL[:], P(Q["kt"]), AF.Copy, scale=EV[:, 1:2]), [Q["kt"], EV], [KTL])
                    k.op('act', lambda e, bcol=bcol: e.activation(VB[:], P(Q["vt"]), AF.Copy, scale=bcol()), [Q["vt"], BE], [VB])
                    k.op('dve', lambda e, qT=qT: e.tensor_tensor(QD[:], qT(), EGB[:], ALU.mult), [Y, EGB], [QD])
                    gcut(45)
                    k.op('pe', lambda e, na=na: e.transpose(P(Q["trA"]), na[:], I_()), [na, CST], [Q["trA"]])
                    k.op('pe', lambda e: e.transpose(P(Q["trB"]), ATT[:], I_()), [ATT, CST], [Q["trB"]])
                    k.op('act', lambda e, nb_=nb_: e.activation(nb_[:], P(Q["trA"]), AF.Copy), [Q["trA"]], [nb_])
                    k.op('dve', lambda e, nb_=nb_, pq=pq: e.tensor_tensor(pq[:], nb_[:], I_(), ALU.add), [nb_, CST], [pq])
                    k.op('act', lambda e: e.activation(ATTt[:], P(Q["trB"]), AF.Copy), [Q["trB"]], [ATTt])
                    gcut(5)
                    cur = 0
                    for lev in range(6):
                        na, nb_, pq = NA[cur], NB[cur], PP[cur]
                        na2, nb2, pq2 = NA[1 - cur], NB[1 - cur], PP[1 - cur]
                        k.op('pe', lambda e, na=na, nb_=nb_: e.matmul(P(Q["Na"]), nb_[:], na[:], start=True, stop=True), [na, nb_], [Q["Na"]])
                        if lev < 5:
                            k.op('pe', lambda e, na=na, nb_=nb_: e.matmul(P(Q["Nb"]), na[:], nb_[:], start=True, stop=True), [na, nb_], [Q["Nb"]])
                        k.op('act', lambda e, na2=na2: e.activation(na2[:], P(Q["Na"]), AF.Copy), [Q["Na"]], [na2])
                        if lev < 5:
                            k.op('act', lambda e, nb2=nb2: e.activation(nb2[:], P(Q["Nb"]), AF.Copy), [Q["Nb"]], [nb2])
                        k.op('pe', lambda e, na2=na2, pq=pq: e.matmul(P(Q["Pc"]), na2[:], pq[:], start=True, stop=True), [na2, pq], [Q["Pc"]])
                        k.op('dve', lambda e, pq=pq, pq2=pq2: e.tensor_tensor(pq2[:], P(Q["Pc"]), pq[:], ALU.add), [Q["Pc"], pq], [pq2])
                        cur = 1 - cur
                    TTm = PP[cur]
                    k.op('pe', lambda e, TTm=TTm: e.matmul(P(Q["u"]), TTm[:], VB[:], start=True, stop=True), [TTm, VB], [Q["u"]])
                    k.op('pe', lambda e, TTm=TTm: e.matmul(P(Q["wT"]), KTG[:], TTm[:], start=True, stop=True), [TTm, KTG], [Q["wT"]])
                    k.op('act', lambda e: e.activation(UU[:], P(Q["u"]), AF.Copy), [Q["u"]], [UU])
                    k.op('dve', lambda e: e.tensor_copy(WT[:], P(Q["wT"])), [Q["wT"]], [WT])
                    gcut(6)
                    S = ST[hh]
                    blk = j * SUB + s
                    if blk % BPS == 0:
                        k.op('dve', lambda e, S=S: e.memset(S[:], 0.0), [], [S])
                    k.op('pe', lambda e, S=S: e.matmul(P(Q["p1"]), WT[:], S[:], start=True, stop=True), [WT, S], [Q["p1"]])
                    k.op('dve', lambda e: e.tensor_tensor(VN[:], UU[:], P(Q["p1"]), ALU.subtract), [UU, Q["p1"]], [VN])
                    k.op('pe', lambda e, S=S: e.matmul(P(Q["po"]), QD[:], S[:], start=True, stop=False), [QD, S], [Q["po"]], signal=False)
                    k.op('pe', lambda e: e.matmul(P(Q["po"]), ATTt[:], VN[:], start=False, stop=True), [ATTt, VN], [Q["po"]])
                    k.op('pe', lambda e: e.matmul(P(ps_s), KTL[:], VN[:], start=True, stop=True), [KTL, VN], [ps_s])
                    k.op('dve', lambda e, S=S: e.scalar_tensor_tensor(S[:], S[:], EV[:, 2:3], P(ps_s), ALU.mult, ALU.add),
                         [S, EV, ps_s], [S])
                    gcut(7)
                    k.op('act', lambda e: e.activation(JK[:], P(Q["po"]), AF.Square, accum_out=OS[:, 0:1]), [Q["po"]], [JK, OS])
                    k.op('act', lambda e: e.activation(OS[:, 1:2], OS[:, 0:1], AF.Sqrt, bias=EPSC[:, 0:1], scale=1.0 / 128), [OS, EPSC], [OS])
                    k.op('dve', lambda e: e.reciprocal(OS[:, 2:3], OS[:, 1:2]), [OS], [OS])
                    k.op('dve', lambda e: e.scalar_tensor_tensor(ON[:], P(Q["po"]), OS[:, 2:3], GN[:], ALU.mult, ALU.mult),
                         [Q["po"], OS, GN], [ON])
                    k.op('pe', lambda e: e.transpose(P(Q["onT"]), ON[:], I_()), [ON, CST], [Q["onT"]])
                    k.op('dve', lambda e, hh=hh, tok0=tok0, gob=gob: e.tensor_tensor(
                        gob[:, hh, tok0:tok0 + 128], P(Q["onT"]), SZ[:, hh, tok0:tok0 + 128], ALU.mult), [Q["onT"], SZ], [gob])
                    gcut(8)
            gcut(9)
            k.dma('sp', [(g_part.t.ap()[256:512, j * TT:(j + 1) * TT].rearrange("(m p) t -> p m t", p=128), gob[:])],
                  [gob], [g_part], gob)
            gcut(10)
        k.allgather(g_part, g_full)

    def phase_merge(l):
        nonlocal off
        off = base_off
        NTm = 256
        WG = alloc("c_wg", [128, KC, 1024], BF16)
        WA = alloc("c_wa", [128, 16, 512], BF16)
        WB = alloc("c_wb", [128, 16, 512], BF16)
        HTb = [alloc(f"c_ht{i}", [128, KC, NTm], BF16) for i in range(2)]
        GTb = [alloc(f"c_gt{i}", [128, KC, NTm], BF16) for i in range(2)]
        SA = alloc("c_sa", [128, NTm], F32)
        SB_ = alloc("c_sb", [128, NTm], F32)
        M1 = alloc("c_m1", [128, NTm], F32)
        M2 = alloc("c_m2", [128, NTm], F32)
        MT = [alloc(f"c_mt{i}", [128, 4, NTm], BF16) for i in range(2)]
        load_w(WG, wg.ap()[l], KC, 1024, 'c_g')
        load_w(WA, wpa.ap()[l], 16, 512, 'c_a')
        load_w(WB, wpb.ap()[l], 16, 512, 'c_b')
        pq = [[psq(2 * (i % 2) + a // 2, (a % 2) * 256, (a % 2 + 1) * 256, f"c_p{i}{a}") for a in range(4)] for i in range(2)]
        ci = 0
        for j in range(NTOK // NTm):
            hb, gb = HTb[j % 2], GTb[j % 2]
            load_hT(hb, hT_full, j * NTm, NTm)
            load_hT(gb, g_full, j * NTm, NTm)
            mt = MT[j % 2]
            for m in range(4):
                pga, pgb, pya, pyb = pq[ci % 2]
                ci += 1
                for kc in range(KC):
                    k.op('pe', lambda e, kc=kc, m=m, hb=hb, pga=pga: e.matmul(
                        P(pga), WG[:, kc, m * 128:(m + 1) * 128], hb[:, kc, :], start=(kc == 0), stop=(kc == KC - 1)),
                        [WG, hb], [pga], signal=(kc == KC - 1))
                for kc in range(KC):
                    k.op('pe', lambda e, kc=kc, m=m, hb=hb, pgb=pgb: e.matmul(
                        P(pgb), WG[:, kc, 512 + m * 128:512 + (m + 1) * 128], hb[:, kc, :], start=(kc == 0), stop=(kc == KC - 1)),
                        [WG, hb], [pgb], signal=(kc == KC - 1))
                for ka in range(16):
                    gk = 4 * (ka // 2) + ka % 2
                    k.op('pe', lambda e, ka=ka, gk=gk, m=m, gb=gb, pya=pya: e.matmul(
                        P(pya), WA[:, ka, m * 128:(m + 1) * 128], gb[:, gk, :], start=(ka == 0), stop=(ka == 15)),
                        [WA, gb], [pya], signal=(ka == 15))
                for kb in range(16):
                    gk = 4 * (kb // 2) + 2 + kb % 2
                    k.op('pe', lambda e, kb=kb, gk=gk, m=m, gb=gb, pyb=pyb: e.matmul(
                        P(pyb), WB[:, kb, m * 128:(m + 1) * 128], gb[:, gk, :], start=(kb == 0), stop=(kb == 15)),
                        [WB, gb], [pyb], signal=(kb == 15))
                k.op('act', lambda e, pga=pga: e.activation(SA[:], P(pga), AF.Sigmoid), [pga], [SA])
                k.op('act', lambda e, pgb=pgb: e.activation(SB_[:], P(pgb), AF.Sigmoid), [pgb], [SB_])
                k.op('dve', lambda e, pya=pya: e.tensor_tensor(M1[:], P(pya), SA[:], ALU.mult), [pya, SA], [M1])
                k.op('dve', lambda e, pyb=pyb: e.tensor_tensor(M2[:], P(pyb), SB_[:], ALU.mult), [pyb, SB_], [M2])
                k.op('dve', lambda e, m=m, mt=mt: e.tensor_tensor(mt[:, m, :], M1[:], M2[:], ALU.add), [M1, M2], [mt])
            k.dma('sp', [(m_part.t.ap()[:, j * NTm:(j + 1) * NTm].rearrange("(m p) t -> p m t", p=128), mt[:])],
                  [mt], [m_part], mt)
        k.allgather(m_part, m_full)
        k.barrier()
        k.release([b for b in list(k.live) if b.name.startswith("c_")])

    def phase_out(l):
        nonlocal off
        off = base_off
        WO = alloc("d_wo", [128, KC, 512], BF16)
        MTb = [alloc(f"d_mt{i}", [128, KC, TT], BF16) for i in range(2)]
        OB = [alloc(f"d_ob{i}", [128, 512], F32) for i in range(2)]
        JK = alloc("d_jk", [128, 512], F32)
        SS = alloc("d_ss", [128, NT128], F32)
        load_w(WO, wo.ap()[l], KC, 512, 'd_w')
        k.op('dve', lambda e: e.memset(SS[:], 0.0), [], [SS])
        pp = [psq(0, 0, 512, "d_p0"), psq(1, 0, 512, "d_p1")]
        for j in range(NTT):
            mb = MTb[j % 2]
            load_hT(mb, m_full, j * TT, TT)
            for s in range(SUB):
                i = j * SUB + s
                ps, ob = pp[i % 2], OB[i % 2]
                for kc in range(KC):
                    k.op('pe', lambda e, kc=kc, s=s, mb=mb, ps=ps: e.matmul(
                        ps.t[:, 0:512], mb[:, kc, s * 128:(s + 1) * 128], WO[:, kc, :], start=(kc == 0), stop=(kc == KC - 1)),
                        [WO, mb], [ps], signal=(kc == KC - 1))
                k.op('dve', lambda e, ps=ps, ob=ob: e.tensor_copy(ob[:], ps.t[:, 0:512]), [ps], [ob])
                k.op('act', lambda e, ob=ob, i=i: e.activation(JK[:], ob[:], AF.Square, accum_out=SS[:, i:i + 1]),
                     [ob], [JK, SS])
                k.dma('sp', [(o_part.t.ap()[i * 128:(i + 1) * 128, :], ob[:])], [ob], [o_part], ob)
        k.dma('sp', [(ssq_part.t.ap(), SS[:])], [SS], [ssq_part], SS)
        k.allgather(ssq_part, ssq_full)
        k.barrier()
        k.release([b for b in list(k.live) if b.name.startswith("d_")])

    def phase_resid(l, src_buf, dst_buf, need_ssq):
        nonlocal off
        off = base_off
        R8 = alloc("e_r8", [128, NCORES, NT128], F32)
        RS = alloc("e_rs", [128, NT128], F32)
        GP = alloc("e_gp", [128, 512], F32)
        XT = [alloc(f"e_x{i}", [128, 512], F32) for i in range(2)]
        OT = [alloc(f"e_o{i}", [128, 512], F32) for i in range(2)]
        XN = [alloc(f"e_xn{i}", [128, 512], F32) for i in range(2)]
        JK = alloc("e_jk", [128, 512], F32)
        SS = alloc("e_ss", [128, NT128], F32)
        load_rstd(R8, RS)
        k.dma('sp', [(GP[:], gpost.ap()[l])], [], [GP], GP)
        k.op('dve', lambda e: e.memset(SS[:], 0.0), [], [SS])
        for i in range(NT128):
            xb, ob, xn = XT[i % 2], OT[i % 2], XN[i % 2]
            k.dma('sp', [(xb[:], src_buf.t.ap()[i * 128:(i + 1) * 128, :])], [src_buf], [xb], xb)
            k.dma('sp', [(ob[:], o_part.t.ap()[i * 128:(i + 1) * 128, :])], [o_part], [ob], ob)
            k.op('dve', lambda e, ob=ob, i=i: e.scalar_tensor_tensor(ob[:], ob[:], RS[:, i:i + 1], GP[:], ALU.mult, ALU.mult),
                 [ob, RS, GP], [ob])
            k.op('dve', lambda e, ob=ob, xb=xb, xn=xn: e.tensor_tensor(xn[:], ob[:], xb[:], ALU.add), [ob, xb], [xn])
            if need_ssq:
                k.op('act', lambda e, xn=xn, i=i: e.activation(JK[:], xn[:], AF.Square, accum_out=SS[:, i:i + 1]), [xn], [JK, SS])
            k.dma('sp', [(dst_buf.t.ap()[i * 128:(i + 1) * 128, :], xn[:])], [xn], [dst_buf], xn)
        if need_ssq:
            k.dma('sp', [(ssq_part.t.ap(), SS[:])], [SS], [ssq_part], SS)
            k.allgather(ssq_part, ssq_full)
        k.barrier()
        k.release([b for b in list(k.live) if b.name.startswith("e_")])

    def dbg_copy(name, src):
        if DEBUG and name in dbg:
            d = dbg[name]
            k.dma('sp', [(d.t.ap(), src.t.ap())], [src], [d], d)

    if plan is not None:
        nm = plan["name"]
        if nm == "S":
            phase_ssq(x_src, 0)
        elif nm == "A":
            phase_hT(x_src, 0)
        elif nm == "B":
            phase_attn(0)
            phase_gdn(0)
        elif nm == "Ba":
            phase_attn(0)
        elif nm == "Bg":
            phase_gdn(0)
        elif nm == "C":
            phase_merge(0)
        elif nm == "D":
            phase_out(0)
        elif nm == "E":
            phase_resid(0, x_src, y_out, need_ssq=True)
    else:
        cur = x_src
        stop = False
        phase_ssq(cur, 0)
        for l in range(DEPTH):
            if STOP_AFTER == "ssq":
                break
            phase_hT(cur, l)
            if l == 0:
                dbg_copy("d_hT", hT_full)
            if STOP_AFTER == "hT":
                break
            phase_attn(l)
            if STOP_AFTER == "attn":
                k.allgather(g_part, g_full)
                dbg_copy("d_g", g_full)
                break
            phase_gdn(l)
            if l == 0:
                dbg_copy("d_g", g_full)
            if STOP_AFTER == "gdn":
                break
            phase_merge(l)
            if l == 0:
                dbg_copy("d_m", m_full)
            if STOP_AFTER == "merge":
                break
            phase_out(l)
            if l == 0:
                dbg_copy("d_o", o_part)
            last = (l == DEPTH - 1)
            dst = y_out if last else xs
            phase_resid(l, cur, dst, need_ssq=not last)
            cur = dst

    outs = [b for b in ext_outs if b._t is not None]
    k.finish_waits('sp', outs)
    k.finish_waits('pool', outs)

    with nc.Block() as block:
        @block.tensor
        def _(e):
            for f in k.q['pe']:
                f(e)

        @block.scalar
        def _(e):
            for f in k.q['act']:
                f(e)

        @block.vector
        def _(e):
            for f in k.q['dve']:
                f(e)

        @block.gpsimd
        def _(e):
            for f in k.q['pool']:
                f(e)

        @block.sync
        def _(e):
            for f in k.q['sp']:
                f(e)
    return nc


def _consts(c):
    H_A = D // 128
    I = np.eye(128, dtype=np.float32)
    p = np.arange(128)
    U = (p[:, None] <= p[None, :]).astype(np.float32)
    Ls = (p[:, None] > p[None, :]).astype(np.float32)
    ones = np.ones((128, 128), np.float32)
    MB = np.where(p[:, None] > p[None, :], 0.0, NEG).astype(np.float32)
    cst = np.stack([I, U, Ls, ones, MB], axis=1).astype(np.float32)
    slopes = np.exp2(-8.0 * np.arange(1, H_A + 1, dtype=np.float32) / H_A).astype(np.float32)
    kk = p[:, None].astype(np.float32)
    qq = p[None, :].astype(np.float32)
    ab = np.zeros((128, 2, 4, 128), np.float32)
    for h in range(4):
        m = slopes[4 * c + h]
        ab[:, 0, h, :] = np.where(qq >= kk, -m * (qq - kk), NEG)
        ab[:, 1, h, :] = np.where(kk > qq, -m * (qq + 128.0 - kk), NEG)
    return cst, ab


def _shard_inputs(x, pre_norm_g, w_in, sinks, conv_w, a_log, dt_bias, gdn_norm_g, w_pa, w_pb, w_o, post_norm_g):
    NTOK = BATCH * SEQ
    xf = np.asarray(x, np.float32).reshape(NTOK, D)
    WIDTH_A = 2048
    o_q, o_k, o_v, o_z = 0, 2048, 2304, 2560
    o_b = 4608
    o_zb = 4608 + 6144
    o_beta = o_zb + 2048
    o_alpha = o_beta + 16
    o_ga = o_alpha + 16
    o_gb = o_ga + 4096
    maps = []
    rep = lambda a: np.ascontiguousarray(np.broadcast_to(a[:, None, :], (a.shape[0], 128, a.shape[1])))
    for c in range(NCORES):
        kv = c // 2
        sl = lambda o, n, i: w_in[:, :, o + n * i:o + n * i + n]
        w2a = np.concatenate([sl(o_q, 256, c), sl(o_k, 64, kv), sl(o_k, 64, kv), sl(o_z, 256, c), sl(o_v, 64, kv)], axis=2)
        w2b = np.concatenate([sl(o_b, 256, c), sl(o_b + 2048, 256, c), sl(o_b + 4096, 256, c), sl(o_zb, 256, c),
                              sl(o_beta, 2, c), sl(o_alpha, 2, c)], axis=2)
        wgc = np.concatenate([sl(o_ga, 512, c), sl(o_gb, 512, c)], axis=2)
        cw = conv_w[:, :, :]
        cols = np.concatenate([np.arange(256 * c, 256 * c + 256), 2048 + np.arange(256 * c, 256 * c + 256),
                               4096 + np.arange(256 * c, 256 * c + 256)])
        cwc = cw[:, :, cols].reshape(DEPTH, 4, 6, 128)
        cwc = np.ascontiguousarray(np.transpose(cwc, (0, 3, 2, 1))).reshape(DEPTH, 128, 24)
        cst, ab = _consts(c)
        maps.append({
            "x": np.ascontiguousarray(xf[:, 512 * c:512 * c + 512]),
            "w2a": np.ascontiguousarray(w2a), "w2b": np.ascontiguousarray(w2b), "wg": np.ascontiguousarray(wgc),
            "wpa": np.ascontiguousarray(w_pa[:, :, 512 * c:512 * c + 512]),
            "wpb": np.ascontiguousarray(w_pb[:, :, 512 * c:512 * c + 512]),
            "wo": np.ascontiguousarray(w_o[:, :, 512 * c:512 * c + 512]),
            "gpre": np.ascontiguousarray(np.transpose(pre_norm_g[:, 512 * c:512 * c + 512].reshape(DEPTH, 4, 128), (0, 2, 1))),
            "gpost": rep(post_norm_g[:, 512 * c:512 * c + 512]),
            "sinks": rep(sinks[:, 4 * c:4 * c + 4]),
            "convw": cwc,
            "alog": rep(a_log[:, 2 * c:2 * c + 2]),
            "dtb": rep(dt_bias[:, 2 * c:2 * c + 2]),
            "gng": rep(gdn_norm_g),
            "cst": cst, "abias": ab,
        })
    return maps


PLANS = {
    "S": {"name": "S", "ins": set(), "outs": {"ssq_part"}},
    "A": {"name": "A", "ins": {"ssq_full"}, "outs": {"hT_part"}},
    "B": {"name": "B", "ins": {"hT_full"}, "outs": {"g_part"}},
    "Ba": {"name": "Ba", "ins": {"hT_full"}, "outs": {"g_part"}},
    "Bg": {"name": "Bg", "ins": {"hT_full"}, "outs": {"g_part"}},
    "C": {"name": "C", "ins": {"hT_full", "g_full"}, "outs": {"m_part"}},
    "D": {"name": "D", "ins": {"m_full"}, "outs": {"o_part", "ssq_part"}},
    "E": {"name": "E", "ins": {"o_part", "ssq_full"}, "outs": {"y", "ssq_part"}},
}
FUSED = False
_NC_CACHE = {}


def _run(name, maps):
    key = (name, SEQ, DEPTH)
    if key not in _NC_CACHE:
        _NC_CACHE[key] = build(PLANS[name])
    res = run_bass_kernel_spmd(_NC_CACHE[key], maps, core_ids=list(range(NCORES)))
    return res.results


def kernel(x, pre_norm_g, w_in, sinks, conv_w, a_log, dt_bias, gdn_norm_g, w_pa, w_pb, w_o, post_norm_g):
    args = [np.asarray(a, np.float32) for a in (x, pre_norm_g, w_in, sinks, conv_w, a_log, dt_bias, gdn_norm_g,
                                                 w_pa, w_pb, w_o, post_norm_g)]
    base = _shard_inputs(*args)
    R = range(NCORES)
    if FUSED:
        nc = build()
        res = run_bass_kernel_spmd(nc, base, core_ids=list(R))
        out = np.concatenate([res.results[c]["y"] for c in R], axis=1)
        return out.reshape(BATCH, SEQ, D).astype(np.float32)
    cc = [{"cst": base[c]["cst"], "abias": base[c]["abias"]} for c in R]
    xcur = [base[c]["x"] for c in R]
    cat0 = lambda r, key: np.concatenate([r[c][key] for c in R], axis=0)
    r = _run("S", [dict(cc[c], x=xcur[c]) for c in R])
    ssq_full = cat0(r, "ssq_part")
    for l in range(DEPTH):
        sl = lambda key, c: np.ascontiguousarray(base[c][key][l:l + 1])
        r = _run("A", [dict(cc[c], x=xcur[c], ssq_full=ssq_full, gpre=sl("gpre", c)) for c in R])
        hT_full = cat0(r, "hT_part")
        r = _run("B", [dict(cc[c], hT_full=hT_full, w2a=sl("w2a", c), w2b=sl("w2b", c), sinks=sl("sinks", c),
                            convw=sl("convw", c), alog=sl("alog", c), dtb=sl("dtb", c), gng=sl("gng", c)) for c in R])
        g_full = cat0(r, "g_part")
        r = _run("C", [dict(cc[c], hT_full=hT_full, g_full=g_full, wg=sl("wg", c), wpa=sl("wpa", c),
                            wpb=sl("wpb", c)) for c in R])
        m_full = cat0(r, "m_part")
        del hT_full, g_full
        r = _run("D", [dict(cc[c], m_full=m_full, wo=sl("wo", c)) for c in R])
        o_parts = [r[c]["o_part"] for c in R]
        ssq_full = cat0(r, "ssq_part")
        del m_full
        r = _run("E", [dict(cc[c], o_part=o_parts[c], ssq_full=ssq_full, x=xcur[c], gpost=sl("gpost", c)) for c in R])
        xcur = [np.asarray(r[c]["y"], np.float32) for c in R]
        ssq_full = cat0(r, "ssq_part")
    out = np.concatenate(xcur, axis=1)
    return out.reshape(BATCH, SEQ, D).astype(np.float32)
```
